# Optimizing a Trainium2 kernel written in Bass

```python
import jax, jax.numpy as jnp
from jax import lax
import numpy as np

D_MODEL = 1024
BATCH = 16
SEQ = 2048
DEPTH = 1

CHUNK = 64
D_HEAD = 64
A_HEADS = 8
A_PREV_CHUNKS = 8
MAX_REL = 128
B_Q_HEADS = 8
B_KV_HEADS = 2
B_GROUP = B_Q_HEADS // B_KV_HEADS
B_WINDOW = 128
B_PREV_CHUNKS = B_WINDOW // CHUNK
A_WIDTH = A_HEADS * D_HEAD
B_Q_WIDTH = B_Q_HEADS * D_HEAD
B_KV_WIDTH = B_KV_HEADS * D_HEAD
D_FF = 2816
REL_TABLE = (CHUNK - 1) + MAX_REL + 1
IN_WIDTH = 3 * A_WIDTH + B_Q_WIDTH + 2 * B_KV_WIDTH + 2 * D_MODEL
EPS = 1e-6
NEG_INF = -1e30

kernel_name = "streaming_hybrid_gated_chunk_attention_block"


def rms_norm(x, g):
    xf = x.astype(jnp.float32)
    y = xf * lax.rsqrt(jnp.mean(xf * xf, axis=-1, keepdims=True) + EPS)
    return y.astype(x.dtype) * g


def swiglu(h, w_gate, w_up, w_down):
    return (jax.nn.silu(h @ w_gate) * (h @ w_up)) @ w_down


def alibi_slopes(n):
    return np.array([2.0 ** (-8.0 * (i + 1) / n) for i in range(n)], dtype=np.float32)


def banded_chunk_attention(q, k, v, n_prev, bias, sinks=None):
    b, s, hkv, g, dh = q.shape
    n_chunks = s // CHUNK
    pad = n_prev * CHUNK
    band = (n_prev + 1) * CHUNK
    scale = 1.0 / np.sqrt(dh)
    kp = jnp.pad(k, ((0, 0), (pad, 0), (0, 0), (0, 0)))
    vp = jnp.pad(v, ((0, 0), (pad, 0), (0, 0), (0, 0)))
    key_off = jnp.arange(band)

    def one_chunk(c):
        start = c * CHUNK
        qc = lax.dynamic_slice_in_dim(q, start, CHUNK, axis=1)
        kc = lax.dynamic_slice_in_dim(kp, start, band, axis=1)
        vc = lax.dynamic_slice_in_dim(vp, start, band, axis=1)
        sc = jnp.einsum('bqhgd,bkhd->bhgqk', qc, kc).astype(jnp.float32) * scale + bias
        valid = (start - pad + key_off) >= 0
        sc = jnp.where(valid, sc, NEG_INF)
        if sinks is not None:
            sink_col = jnp.broadcast_to(sinks.astype(jnp.float32)[None, :, :, None, None],
                                        sc.shape[:-1] + (1,))
            p = jax.nn.softmax(jnp.concatenate([sc, sink_col], axis=-1), axis=-1)[..., :band]
        else:
            p = jax.nn.softmax(sc, axis=-1)
        return jnp.einsum('bhgqk,bkhd->bqhgd', p.astype(vc.dtype), vc)

    out = lax.map(one_chunk, jnp.arange(n_chunks))
    return jnp.moveaxis(out, 0, 1).reshape(b, s, hkv * g * dh)


def setup_inputs(seed: int = 0) -> dict:
    key = jax.random.key(seed)
    ks = jax.random.split(key, 20)
    f32 = jnp.float32

    def w(k, shape, fan_in):
        return jax.random.normal(k, shape, f32) * fan_in ** -0.5

    def gain(k):
        return 1.0 + 0.01 * jax.random.normal(k, (D_MODEL,), f32)

    return {
        "x": jax.random.normal(ks[0], (BATCH, SEQ, D_MODEL), f32),
        "ffn1_norm": gain(ks[1]),
        "ffn1_w_gate": w(ks[2], (D_MODEL, D_FF), D_MODEL),
        "ffn1_w_up": w(ks[3], (D_MODEL, D_FF), D_MODEL),
        "ffn1_w_down": w(ks[4], (D_FF, D_MODEL), D_FF),
        "mix_norm": gain(ks[5]),
        "w_in": w(ks[6], (D_MODEL, IN_WIDTH), D_MODEL),
        "rel_bias": 0.5 * jax.random.normal(ks[7], (A_HEADS, REL_TABLE), f32),
        "sinks": jax.random.normal(ks[8], (B_Q_HEADS,), f32),
        "w_proj_a": w(ks[9], (A_WIDTH, D_MODEL), A_WIDTH),
        "w_proj_b": w(ks[10], (B_Q_WIDTH, D_MODEL), B_Q_WIDTH),
        "w_out": w(ks[11], (D_MODEL, D_MODEL), D_MODEL),
        "ffn2_norm": gain(ks[12]),
        "ffn2_w_gate": w(ks[13], (D_MODEL, D_FF), D_MODEL),
        "ffn2_w_up": w(ks[14], (D_MODEL, D_FF), D_MODEL),
        "ffn2_w_down": w(ks[15], (D_FF, D_MODEL), D_FF),
        "final_norm": gain(ks[16]),
    }


def reference(x, ffn1_norm, ffn1_w_gate, ffn1_w_up, ffn1_w_down, mix_norm, w_in,
              rel_bias, sinks, w_proj_a, w_proj_b, w_out, ffn2_norm, ffn2_w_gate,
              ffn2_w_up, ffn2_w_down, final_norm):
    b, s, _ = x.shape

    qi = np.arange(CHUNK)[:, None]
    kj_a = np.arange((A_PREV_CHUNKS + 1) * CHUNK)[None, :]
    rel_a = qi - kj_a + A_PREV_CHUNKS * CHUNK
    idx_a = np.clip(rel_a, -(CHUNK - 1), MAX_REL) + (CHUNK - 1)
    kj_b = np.arange((B_PREV_CHUNKS + 1) * CHUNK)[None, :]
    dist_b = np.abs(qi - kj_b + B_PREV_CHUNKS * CHUNK).astype(np.float32)
    slopes = jnp.asarray(alibi_slopes(B_Q_HEADS)).reshape(B_KV_HEADS, B_GROUP)

    for _layer in range(DEPTH):
        x = x + 0.5 * swiglu(rms_norm(x, ffn1_norm), ffn1_w_gate, ffn1_w_up, ffn1_w_down)

        h = rms_norm(x, mix_norm)
        proj = h @ w_in
        cuts = np.cumsum([A_WIDTH, A_WIDTH, A_WIDTH, B_Q_WIDTH, B_KV_WIDTH, B_KV_WIDTH, D_MODEL])
        qa, ka, va, qb, kb, vb, gate_a, gate_b = jnp.split(proj, cuts, axis=-1)

        bias_a = rel_bias[:, idx_a].astype(jnp.float32)[:, None]
        ya = banded_chunk_attention(
            qa.reshape(b, s, A_HEADS, 1, D_HEAD),
            ka.reshape(b, s, A_HEADS, D_HEAD),
            va.reshape(b, s, A_HEADS, D_HEAD),
            A_PREV_CHUNKS, bias_a)
        ya = ya @ w_proj_a

        bias_b = -slopes[:, :, None, None] * jnp.asarray(dist_b)[None, None]
        yb = banded_chunk_attention(
            qb.reshape(b, s, B_KV_HEADS, B_GROUP, D_HEAD),
            kb.reshape(b, s, B_KV_HEADS, D_HEAD),
            vb.reshape(b, s, B_KV_HEADS, D_HEAD),
            B_PREV_CHUNKS, bias_b, sinks.reshape(B_KV_HEADS, B_GROUP))
        yb = yb @ w_proj_b

        merged = jax.nn.sigmoid(gate_a) * ya + jax.nn.sigmoid(gate_b) * yb
        x = x + merged @ w_out

        x = x + 0.5 * swiglu(rms_norm(x, ffn2_norm), ffn2_w_gate, ffn2_w_up, ffn2_w_down)

    return rms_norm(x, final_norm)
```

```python
import numpy as np
import concourse.bass as bass
import concourse.mybir as mybir
from concourse.bass_utils import run_bass_kernel_spmd

F32 = mybir.dt.float32
BF16 = mybir.dt.bfloat16
AF = mybir.ActivationFunctionType
ALU = mybir.AluOpType

NCORES = 8
D = 1024
SEQ = 2048
DFF = 2816
NG = 11
NSUB = 4
TS = 512
EPS = 1e-6
NEG = -30000.0
PIECE = 6144
NSLOT = 2
SEQ_PER_CORE = 2

ENGS = ("pe", "act", "dve", "pool", "sp")


class Sched:
    def __init__(self):
        self.ops = []
        self.count = {e: 0 for e in ENGS}
        self.dma_count = {}
        self.last_w = {}
        self.readers = {}
        self.seen = {e: {} for e in ENGS}
        self.ps_rr = 0

    def ps(self):
        i = self.ps_rr
        self.ps_rr = (self.ps_rr + 1) % 8
        return i

    def alias(self, dst, src):
        best = {}
        for k in src:
            refs = list(self.readers.get(k, ()))
            if k in self.last_w:
                refs.append(self.last_w[k])
            for (semkey, val, reng) in refs:
                if semkey not in best or best[semkey][1] < val:
                    best[semkey] = (semkey, val, reng)
        for d in dst:
            self.readers.setdefault(d, []).extend(best.values())

    def _add(self, eng, fn, reads, writes, dma=None):
        deps = {}

        def need(ref, raw):
            semkey, val, reng = ref
            if reng == eng and dma is None and not semkey.startswith("dma"):
                if eng == "pe":
                    return
            if deps.get(semkey, 0) < val:
                deps[semkey] = val

        for k in reads:
            w = self.last_w.get(k)
            if w is not None:
                need(w, True)
        for k in writes:
            w = self.last_w.get(k)
            if w is not None:
                need(w, False)
            for r in self.readers.get(k, ()):
                need(r, False)
        waits = []
        seen = self.seen[eng]
        for semkey, val in deps.items():
            if seen.get(semkey, 0) < val:
                seen[semkey] = val
                waits.append((semkey, val))
        if dma is not None:
            v = self.dma_count.get(dma, 0) + 16
            self.dma_count[dma] = v
            ref = (dma, v, eng)
            inc = (dma, 16)
        else:
            self.count[eng] += 1
            ref = (eng, self.count[eng], eng)
            inc = (eng, 1)
        for k in reads:
            self.readers.setdefault(k, []).append(ref)
        for k in writes:
            self.last_w[k] = ref
            self.readers[k] = []
        self.ops.append((eng, fn, waits, inc))
        return ref

    def mm(self, out, lhsT, rhs, start, stop, reads, writes, skip=False, tp=None):
        if skip:
            if tp is not None:
                return self._add("pe", lambda e: e.matmul(out, lhsT=lhsT, rhs=rhs, start=start, stop=stop,
                                                          skip_group_check=True, tile_position=tp), reads, writes)
            return self._add("pe", lambda e: e.matmul(out, lhsT=lhsT, rhs=rhs, start=start, stop=stop,
                                                      skip_group_check=True), reads, writes)
        return self._add("pe", lambda e: e.matmul(out, lhsT=lhsT, rhs=rhs, start=start, stop=stop),
                         reads, writes)

    def tr(self, out, in_, ident, reads, writes):
        return self._add("pe", lambda e: e.transpose(out, in_, ident), reads, writes)

    def act(self, out, in_, func, reads, writes, bias=None, scale=None):
        kw = {}
        if bias is not None:
            kw["bias"] = bias
        if scale is not None:
            kw["scale"] = scale
        return self._add("act", lambda e: e.activation(out=out, in_=in_, func=func, **kw), reads, writes)

    def tt(self, out, in0, in1, op, reads, writes, eng="dve"):
        return self._add(eng, lambda e: e.tensor_tensor(out=out, in0=in0, in1=in1, op=op), reads, writes)

    def stt(self, out, in0, scalar, in1, op0, op1, reads, writes):
        return self._add("dve", lambda e: e.scalar_tensor_tensor(out=out, in0=in0, scalar=scalar, in1=in1,
                                                                  op0=op0, op1=op1), reads, writes)

    def ts(self, out, in0, s1, op0, reads, writes, s2=None, op1=None):
        if op1 is None:
            return self._add("dve", lambda e: e.tensor_scalar(out=out, in0=in0, scalar1=s1, scalar2=None,
                                                              op0=op0), reads, writes)
        return self._add("dve", lambda e: e.tensor_scalar(out=out, in0=in0, scalar1=s1, scalar2=s2,
                                                          op0=op0, op1=op1), reads, writes)

    def copy(self, out, in_, reads, writes, eng="dve"):
        if eng == "act":
            return self._add("act", lambda e: e.copy(out=out, in_=in_), reads, writes)
        return self._add(eng, lambda e: e.tensor_copy(out=out, in_=in_), reads, writes)

    def recip(self, out, in_, reads, writes):
        return self._add("dve", lambda e: e.reciprocal(out=out, in_=in_), reads, writes)

    def memset(self, ap, val, writes, eng="dve"):
        return self._add(eng, lambda e: e.memset(ap, val), (), writes)

    def dma(self, queue, out, in_, semkey, reads, writes):
        return self._add(queue, lambda e: e.dma_start(out=out, in_=in_), reads, writes, dma=semkey)

    def raw(self, eng, fn, reads, writes):
        return self._add(eng, fn, reads, writes)


def _kmajor(w):
    K = w.shape[0] // 128
    return np.ascontiguousarray(w.reshape(K, 128, w.shape[1]).transpose(1, 0, 2)).reshape(128, -1)


def _ffn_pieces(wg, wu, wd):
    out = np.zeros((NG, 128, PIECE), np.float32)
    for g in range(NG):
        out[g, :, 0:2048] = _kmajor(wg[:, g * 256:(g + 1) * 256])
        out[g, :, 2048:4096] = _kmajor(wu[:, g * 256:(g + 1) * 256])
        out[g, :, 4096:6144] = _kmajor(wd[g * 256:(g + 1) * 256, :])
    return out


def _mixer_pieces(w_in, wpa, wpb, wo):
    out = np.zeros((NMIX, 128, PIECE), np.float32)
    out[0, :, 0:4096] = _kmajor(w_in[:, 0:512])
    out[1, :, 0:4096] = _kmajor(w_in[:, 512:1024])
    out[2, :, 0:4096] = _kmajor(w_in[:, 1024:1536])
    out[3, :, 0:4096] = _kmajor(w_in[:, 1536:2048])
    kb = w_in[:, 2048:2176]
    vb = w_in[:, 2176:2304]
    kvb = np.concatenate([kb[:, 0:64], kb[:, 0:64], kb[:, 64:128], kb[:, 64:128], vb], axis=1)
    out[4, :, 0:3072] = _kmajor(kvb)
    out[5, :, 0:4096] = _kmajor(w_in[:, 2304:2816])
    out[6, :, 0:4096] = _kmajor(w_in[:, 2816:3328])
    for dp in range(4):
        sl = slice(dp * 256, (dp + 1) * 256)
        out[7 + dp, :, 0:2048] = _kmajor(w_in[:, 3328:4352][:, sl])
        out[7 + dp, :, 2048:3072] = _kmajor(wpa[:, sl])
        out[7 + dp, :, 3072:4096] = _kmajor(wpb[:, sl])
    out[11, :, 0:4096] = _kmajor(wo[:, 0:512])
    out[12, :, 0:4096] = _kmajor(wo[:, 512:1024])
    return out


NMIX = 13
MIX_N = [4096, 4096, 4096, 4096, 3072, 4096, 4096, 4096, 4096, 4096, 4096, 4096, 4096]


def _bias_tables(rel_bias):
    p = np.arange(128)[:, None]
    q = np.arange(128)[None, :]
    cq = q // 64
    cp = p // 64
    ta = np.zeros((128, 8, 2, 128), np.float32)
    for r, s in enumerate((4, 3)):
        rel = q - p + (4 - s) * 128
        dc = cq + 8 - 2 * s - cp
        valid = (dc >= 0) & (dc <= 8)
        idx = np.clip(rel, -63, 128) + 63
        for h in range(8):
            ta[:, h, r, :] = np.where(valid, rel_bias[h][idx], np.float32(NEG))
    ca = np.ascontiguousarray(np.broadcast_to(rel_bias[:, 191][None, :], (128, 8))).astype(np.float32)
    tb = np.zeros((128, 8, 2, 128), np.float32)
    for r, s in enumerate((1, 0)):
        dist = np.abs(q - p + (1 - s) * 128).astype(np.float32)
        dc = cq + 2 - 2 * s - cp
        valid = (dc >= 0) & (dc <= 2)
        for h in range(8):
            slope = np.float32(2.0 ** (-8.0 * (h + 1) / 8))
            tb[:, h, r, :] = np.where(valid, -slope * dist, np.float32(NEG))
    return ta, ca, tb


def _gain(g):
    return np.ascontiguousarray(g.reshape(8, 128).T).astype(np.float32)


def build_program():
    nc = bass.Bass("TRN2", target_bir_lowering=False)
    x_d = nc.dram_tensor("x", [SEQ_PER_CORE, SEQ, D], F32, kind="ExternalInput").ap()
    y_d = nc.dram_tensor("y", [SEQ_PER_CORE, SEQ, D], F32, kind="ExternalOutput").ap()
    wf1_d = nc.dram_tensor("wf1", [NG, 128, PIECE], F32, kind="ExternalInput").ap()
    wf2_d = nc.dram_tensor("wf2", [NG, 128, PIECE], F32, kind="ExternalInput").ap()
    wmx_d = nc.dram_tensor("wmx", [NMIX, 128, PIECE], F32, kind="ExternalInput").ap()
    gains_d = nc.dram_tensor("gains", [128, 3 * 8], F32, kind="ExternalInput").ap()
    gfin_d = nc.dram_tensor("gfin", [128, D], F32, kind="ExternalInput").ap()
    ta_d = nc.dram_tensor("ta", [128, 8 * 2 * 128], F32, kind="ExternalInput").ap()
    ca_d = nc.dram_tensor("ca", [128, 8], F32, kind="ExternalInput").ap()
    tb_d = nc.dram_tensor("tb", [128, 8 * 2 * 128], F32, kind="ExternalInput").ap()
    snk_d = nc.dram_tensor("snk", [128, 4], F32, kind="ExternalInput").ap()

    wmb_d = nc.dram_tensor("wmx_bf16", [NMIX, 128, PIECE], BF16, kind="Internal").ap()

    S = Sched()
    from contextlib import ExitStack
    with ExitStack() as es:
        def sb(name, shape, dt):
            return es.enter_context(nc.sbuf_tensor(name, shape, dt))

        xT = sb("xT", [128, 8, SEQ], F32)
        wsl = [sb("wsl%d" % i, [128, PIECE], BF16) for i in range(NSLOT)]
        ident = sb("ident", [128, 128], F32)
        ones = sb("ones", [128, 128], BF16)
        mhalf_t = sb("mhalf", [128, 1], F32)
        mhalf = mhalf_t[:, 0:1].to_broadcast([128, TS])
        gains = sb("gains_s", [128, 24], F32)
        gfin = sb("gfin_s", [128, D], F32)
        ta = sb("ta_s", [128, 8, 2, 128], F32)
        ca = sb("ca_s", [128, 8], F32)
        tb = sb("tb_s", [128, 8, 2, 128], F32)
        esink = sb("esink", [128, 4], F32)
        ssq = sb("ssq", [128, 8], F32)
        R = sb("R", [128, 8 * SEQ], BF16)
        hT = R[:].rearrange("p (c n) -> p c n", c=8)
        XS = sb("XS", [128, 2 * D], F32)
        xin = [XS[:, i * D:(i + 1) * D] for i in range(2)]
        sq_t = sb("sq", [128, 8 * TS], BF16)
        sq = sq_t[:].rearrange("p (c n) -> p c n", c=8)
        lin = None
        sgA = [XS[:, d_ * TS:(d_ + 1) * TS] for d_ in range(4)] + \
              [sq_t[:].bitcast(F32)[:, d_ * TS:(d_ + 1) * TS] for d_ in range(4)]
        rstd = [sb("rstd%d" % i, [128, TS], F32) for i in range(4)]
        aT = [sb("aT%d" % i, [128, 2, TS], BF16) for i in range(2)]
        scrA = [sb("scrA%d" % i, [128, 2, TS], F32) for i in range(2)]
        scrN = sb("scrN", [128, TS], F32)
        hs = R[:, 0:4096].rearrange("p (c n) -> p c n", c=8)
        qaT = R[:, 4096:6144].rearrange("p (c n) -> p c n", c=4)
        qbT = R[:, 6144:8192].rearrange("p (c n) -> p c n", c=4)
        attA = R[:, 8192:10240].rearrange("p (c n) -> p c n", c=4)
        attB = R[:, 10240:12288].rearrange("p (c n) -> p c n", c=4)
        merged = R[:, 12288:16384].rearrange("p (c n) -> p c n", c=8)
        kaT_t = sb("kaT", [128, 4 * 2 * TS], BF16)
        kaT = kaT_t[:].rearrange("p (c n) -> p c n", c=4)
        lin = [kaT_t[:].bitcast(F32)[:, i * D:(i + 1) * D] for i in range(2)]
        kbT = sb("kbT", [128, 2, 2 * TS], BF16)
        vA_t = sb("vA", [128, 8 * 512], BF16)
        vA = vA_t[:].rearrange("p (a c) -> p a c", a=8)
        lin = lin + [vA_t[:].bitcast(F32)[:, i * D:(i + 1) * D] for i in range(2)]
        vB = sb("vB", [128, 8, 128], BF16)
        NPT = 3
        PT = [sb("PT%d" % i, [128, 2, TS], BF16) for i in range(NPT)]
        pp = [es.enter_context(nc.psum_tensor("pp%d" % i, [128, 2, 512], F32)) for i in range(4)]

        def ps(i):
            return pp[i // 2][:, i % 2, :]

        sem_names = list(ENGS) + ["dma_w%d" % i for i in range(NSLOT)] + \
            ["dma_xin0", "dma_xin1", "dma_xin2", "dma_xin3", "dma_xs2", "dma_xs3", "dma_out0", "dma_out1"] + ["dma_c%d" % i for i in range(6)] + \
            ["dma_cv%d" % i for i in range(NMIX)] + ["dma_wm%d" % i for i in range(NSLOT)] + ["dma_xs0", "dma_xs1"]
        sems = {n: es.enter_context(nc.semaphore(n)) for n in sem_names}

        FFN_KEYS = [("h", t_) for t_ in range(NSUB)]
        MIX_KEYS = ["hs", "qaT", "qbT", "attA", "attB"] + [("mg", d_) for d_ in range(8)]
        SQK = [("sq", 0), ("sq", 1)]
        SGK = [("xin", 0), ("xin", 0), ("xin", 1), ("xin", 1), ("sq", 0), ("sq", 0), ("sq", 1), ("sq", 1)]

        S.dma("sp", gains[:], gains_d, "dma_c0", (), ["gains"])
        S.memset(ones[:], 1.0, ["ones"])
        S.memset(mhalf_t[:], -0.5, ["mhalf"])
        S.memset(ident[:], 0.0, ["ident"], eng="pool")
        S.raw("pool", lambda e: e.affine_select(out=ident[:], in_=ident[:], pattern=[[-1, 128]],
                                                compare_op=ALU.not_equal, fill=1.0, base=0,
                                                channel_multiplier=1), ["ident"], ["ident"])
        def late_setup():
            S.dma("sp", ta[:].rearrange("p a b c -> p (a b c)"), ta_d, "dma_c1", (), ["ta"])
            S.dma("sp", ca[:], ca_d, "dma_c2", (), ["ca"])
            S.dma("sp", tb[:].rearrange("p a b c -> p (a b c)"), tb_d, "dma_c3", (), ["tb"])
            S.dma("sp", esink[:], snk_d, "dma_c4", (), ["esink"])
            S.dma("sp", gfin[:], gfin_d, "dma_c5", (), ["gfin"])
            S.act(esink[:], esink[:], AF.Exp, ["esink"], ["esink"])
            for h in range(8):
                S.ts(ta[:, h, :, :].rearrange("p a b -> p (a b)"), ta[:, h, :, :].rearrange("p a b -> p (a b)"),
                     ca[:, h:h + 1], ALU.subtract, ["ta", "ca"], ["ta"])

        wstate = {"n": 0, "issued": 0, "sched": []}

        def wsched_build():
            sched = []
            for _ in range(SEQ_PER_CORE):
                sched += [(wf1_d, g, PIECE, "f") for g in range(NG)]
                for _t in range(NSUB):
                    sched += [(wmb_d, i, MIX_N[i], "m") for i in range(NMIX)]
                sched += [(wf2_d, g, PIECE, "f") for g in range(NG)]
            return sched
        wstate["sched"] = wsched_build()
        wstate["conv"] = 0

        def w_issue(upto):
            while wstate["issued"] <= upto and wstate["issued"] < len(wstate["sched"]):
                i = wstate["issued"]
                src, idx, n, kd = wstate["sched"][i]
                slot = i % NSLOT
                if kd == "f":
                    S.dma("pool", wsl[slot][:, 0:n], src[idx, :, 0:n], "dma_w%d" % slot,
                          ([("lin", j_) for j_ in range(4)] if i == 0 else ()), [("w", slot)])
                    for _r in range(0 if i < 2 else 2):
                        if wstate["conv"] >= NMIX:
                            break
                        c = wstate["conv"]
                        S.dma("pool", wmb_d[c, :, 0:MIX_N[c]], wmx_d[c, :, 0:MIX_N[c]], "dma_cv%d" % c, (),
                              [("wbf", c)])
                        wstate["conv"] += 1
                else:
                    while wstate["conv"] < NMIX:
                        c = wstate["conv"]
                        S.dma("pool", wmb_d[c, :, 0:MIX_N[c]], wmx_d[c, :, 0:MIX_N[c]], "dma_cv%d" % c, (),
                              [("wbf", c)])
                        wstate["conv"] += 1
                    assert ("wbf", idx) in S.last_w
                    S.dma("sp", wsl[slot][:, 0:n], src[idx, :, 0:n], "dma_wm%d" % slot, [("wbf", idx)],
                          [("w", slot)])
                wstate["issued"] += 1

        def w_acquire(prefetch=True):
            i = wstate["n"]
            wstate["n"] += 1
            w_issue(i + NSLOT - 1 if prefetch else i)
            return i % NSLOT

        def w_prefetch():
            w_issue(wstate["n"] - 1 + NSLOT - 1)

        def tsl(t):
            return slice(t * TS, (t + 1) * TS)

        def norm_sq(t):
            xk = [("x", c, t) for c in range(8)]
            S.act(sq, xT[:, :, tsl(t)], AF.Square, xk, SQK)

        def norm_rest(t, rb):
            p = S.ps()
            for c in range(8):
                S.mm(ps(p), ones[:], sq[:, c, :], c == 0, c == 7, ["ones"] + SQK, [("ps", p)])
            S.act(rstd[rb][:], ps(p), AF.Sqrt, [("ps", p)], [("rstd", rb)], bias=EPS, scale=1.0 / D)
            S.recip(rstd[rb][:], rstd[rb][:], [("rstd", rb)], [("rstd", rb)])

        def norm_rstd(t, rb):
            norm_sq(t)
            norm_rest(t, rb)

        def norm_to(t, gidx, out_fn, out_key, rb):
            norm_rstd(t, rb)
            norm_apply(t, gidx, out_fn, out_key, rb)

        def norm_apply(t, gidx, out_fn, out_key, rb):
            for c in range(8):
                S.stt(out_fn(c), xT[:, c, tsl(t)], gains[:, gidx * 8 + c:gidx * 8 + c + 1], rstd[rb][:],
                      ALU.mult, ALU.mult, [("x", c, t), "gains", ("rstd", rb)], [out_key])

        def load_dma(s, t, q="pool"):
            for jj in range(4):
                i = t * 4 + jj
                S.dma(q, lin[jj], x_d[s, i * 128:(i + 1) * 128, :], ("dma_xin%d" if q == "pool" else "dma_xs%d") % jj,
                      (), [("lin", jj)])

        def load_tile(s, t, jj):
            i = t * 4 + jj
            for hf in range(2):
                p = S.ps()
                for cc in range(4):
                    c = hf * 4 + cc
                    S.tr(ps(p)[:, cc * 128:(cc + 1) * 128], lin[jj][:, c * 128:(c + 1) * 128], ident[:],
                         [("lin", jj), "ident"], [("ps", p)])
                S.copy(xT[:, hf * 4:hf * 4 + 4, i * 128:(i + 1) * 128],
                       ps(p).rearrange("p (c n) -> p c n", c=4),
                       [("ps", p)], [("x", hf * 4 + cc, t) for cc in range(4)],
                       eng=("act" if hf == 0 else "dve"))

        def load_subtile(s, t, q="pool"):
            load_dma(s, t, q)
            for jj in range(4):
                load_tile(s, t, jj)

        def store_subtile(s, t, after_tile=None):
            for jj in range(4):
                store_tile(s, t, jj)
                if after_tile is not None:
                    after_tile(jj)

        def store_tile(s, t, jj):
            if True:
                i = t * 4 + jj
                b = i % 2
                pb = []
                for hf in range(2):
                    p = S.ps()
                    pb.append(p)
                    for cc in range(4):
                        c = hf * 4 + cc
                        S.tr(ps(p)[:, cc * 128:(cc + 1) * 128], xT[:, c, i * 128:(i + 1) * 128], ident[:],
                             [("x", c, t), "ident"], [("ps", p)])
                    S.raw("act", (lambda e, p=p, hf=hf, b=b: e.activation(
                        out=scrN[:, :].bitcast(BF16)[:, 0:TS], in_=ps(p), func=AF.Square,
                        accum_out=ssq[:, b * 4 + hf:b * 4 + hf + 1])), [("ps", p)], ["scrN", ("ssq", b, hf)])
                S.tt(ssq[:, b * 4 + 2:b * 4 + 3], ssq[:, b * 4:b * 4 + 1], ssq[:, b * 4 + 1:b * 4 + 2], ALU.add,
                     [("ssq", b, 0), ("ssq", b, 1)], [("ssq", b, 2)])
                S.ts(ssq[:, b * 4 + 2:b * 4 + 3], ssq[:, b * 4 + 2:b * 4 + 3], 1.0 / D, ALU.mult,
                     [("ssq", b, 2)], [("ssq", b, 2)], s2=EPS, op1=ALU.add)
                S.tt(ssq[:, b * 4 + 3:b * 4 + 4], ssq[:, b * 4 + 2:b * 4 + 3], mhalf_t[:, 0:1], ALU.pow,
                     [("ssq", b, 2), "mhalf"], [("ssq", b, 3)], eng="pool")
                for hf in range(2):
                    S.stt(xin[b][:, hf * 512:(hf + 1) * 512], ps(pb[hf]), ssq[:, b * 4 + 3:b * 4 + 4],
                          gfin[:, hf * 512:(hf + 1) * 512], ALU.mult, ALU.mult,
                          [("ps", pb[hf]), ("ssq", b, 3), "gfin"], [("xin", b)])
                S.dma("sp", y_d[s, i * 128:(i + 1) * 128, :], xin[b], "dma_out%d" % b, [("xin", b)], [])

        def ffn_stage(s, which, on_final=None, before_step=None):
            steps = [(g, t) for g in range(NG) for t in range(NSUB)]
            pend = None
            slot = None

            def flush(pend):
                g_, t_, slot_, ab_ = pend
                down(t_, slot_, ab_)
                if g_ == NG - 1 and on_final is not None:
                    on_final(t_)

            for si, (g, t) in enumerate(steps):
                if t == 0:
                    slot = w_acquire(prefetch=(pend is None))
                    need_pf = pend is not None
                ab = si % 2
                w = wsl[slot]
                if before_step is not None:
                    before_step(g, t)
                for fc in range(2):
                    pg = S.ps()
                    pu = S.ps()
                    for k in range(8):
                        S.mm(ps(pg), w[:, k * 256 + fc * 128:k * 256 + fc * 128 + 128], hT[:, k, tsl(t)],
                             k == 0, k == 7, [("w", slot), ("h", t)], [("ps", pg)])
                    for k in range(8):
                        S.mm(ps(pu), w[:, 2048 + k * 256 + fc * 128:2048 + k * 256 + fc * 128 + 128],
                             hT[:, k, tsl(t)], k == 0, k == 7, [("w", slot), ("h", t)], [("ps", pu)])
                    S.act(scrA[fc][:, 0, :], ps(pg), AF.Silu, [("ps", pg)], [("scr", fc, 0)])
                    S.tt(aT[ab][:, fc, :], scrA[fc][:, 0, :], ps(pu), ALU.mult,
                         [("scr", fc, 0), ("ps", pu)], [("aT", ab, fc)])
                if pend is not None:
                    flush(pend)
                if t == 0 and need_pf:
                    w_prefetch()
                pend = (g, t, slot, ab)
            flush(pend)

        def down(t, slot, ab):
            w = wsl[slot]
            for d in range(8):
                p = S.ps()
                for fc in range(2):
                    S.mm(ps(p), w[:, 4096 + fc * 1024 + d * 128:4096 + fc * 1024 + d * 128 + 128],
                         aT[ab][:, fc, :], fc == 0, fc == 1, [("w", slot), ("aT", ab, fc)], [("ps", p)])
                S.stt(xT[:, d, tsl(t)], ps(p), 0.5, xT[:, d, tsl(t)], ALU.mult, ALU.add,
                      [("ps", p), ("x", d, t)], [("x", d, t)])

        def proj_fm(slot, ncol_per_k, col0, n_oc, out_fn, out_keys):
            w = wsl[slot]
            for oc in range(n_oc):
                p = S.ps()
                for k in range(8):
                    c0 = k * ncol_per_k + col0 + oc * 128
                    S.mm(ps(p), w[:, c0:c0 + 128], hs[:, k, :], k == 0, k == 7,
                         [("w", slot), "hs"], [("ps", p)])
                S.copy(out_fn(oc), ps(p), [("ps", p)], [out_keys(oc)], eng="act")

        def attention(t, fill_a, fill_b):
            kvr = [("kv", 0), ("kv", 1)]
            cfg = {"A": (-4, 5, ta, kaT, qaT, attA), "B": (-1, 2, tb, kbT, qbT, attB)}
            steps = []
            pairs = []
            for kind in ("A", "B"):
                omin, width = cfg[kind][0], cfg[kind][1]
                for m in range(4):
                    os_ = [o for o in range(omin, 4) if 4 * t + o >= 0]
                    first = {}
                    for o in os_:
                        for j in range(max(0, o), min(3, o + width - 1) + 1):
                            first.setdefault(j, o)
                    pi = len(pairs)
                    pairs.append({"kind": kind, "m": m, "first": first, "started": [False, False],
                                  "PO": 6, "PD": 7, "last": os_[-1]})
                    for o in os_:
                        steps.append((pi, o))
            info = {}

            def qk(n):
                pi, o = steps[n]
                P = pairs[pi]
                kind, m = P["kind"], P["m"]
                omin, width, tab, kT, qT, att = cfg[kind]
                sp_ = n % 3
                pt = n % NPT
                sa = n % 2
                pos = (4 * t + o) % 8
                jlo, jhi = max(0, o), min(3, o + width - 1)
                cs = slice(jlo * 128, (jhi + 1) * 128)
                for e in range(2):
                    kc = m if kind == "A" else m // 2
                    S.mm(pp[sp_][:, e, cs], kT[e * 64:(e + 1) * 64, kc, pos * 128:(pos + 1) * 128],
                         qT[e * 64:(e + 1) * 64, m, cs], True, True,
                         kvr + ["qT" + kind], [("ps", 2 * sp_ + e)])
                pkeys = [("ps", 2 * sp_), ("ps", 2 * sp_ + 1)]
                if kind == "B":
                    tj = list(range(jlo, jhi + 1))
                else:
                    tj = [j for j in (o, o + 1) if jlo <= j <= jhi]
                dj = [j for j in range(jlo, jhi + 1) if j not in tj]
                if tj:
                    r0 = tj[0] - o
                    r1 = tj[-1] - o
                    c = slice(tj[0] * 128, (tj[-1] + 1) * 128)
                    S.stt(scrA[sa][:, :, c], pp[sp_][:, :, c], 0.125,
                          tab[:, 2 * m:2 * m + 2, r0:r1 + 1, :].rearrange("p h r q -> p h (r q)"),
                          ALU.mult, ALU.add, pkeys + ["t" + kind.lower()], [("scr", sa, 0), ("scr", sa, 1)])
                    S.act(PT[pt][:, :, c], scrA[sa][:, :, c], AF.Exp, [("scr", sa, 0), ("scr", sa, 1)],
                          [("PT", pt)])
                if dj:
                    c = slice(dj[0] * 128, (dj[-1] + 1) * 128)
                    S.act(PT[pt][:, :, c], pp[sp_][:, :, c], AF.Exp, pkeys, [("PT", pt)], scale=0.125)
                info[n] = (pt, pos, jlo, jhi)

            def pv(n):
                pi, o = steps[n]
                P = pairs[pi]
                kind, m, first, started = P["kind"], P["m"], P["first"], P["started"]
                PO, PD = P["PO"], P["PD"]
                pt, pos, jlo, jhi = info[n]
                last = (o == P["last"])
                for e in range(2):
                    h = 2 * m + e
                    ol = slice(e * 64, (e + 1) * 64)
                    if kind == "A":
                        vcols = slice(h * 64, (h + 1) * 64)
                        lw, lw_hi = vA[:, pos, vcols], vA[64:128, pos, vcols]
                    else:
                        g = m // 2
                        vcols = slice(g * 64, (g + 1) * 64)
                        lw, lw_hi = vB[:, pos, vcols], None
                    st_j = [j for j in range(jlo, jhi + 1) if first[j] == o]
                    ct_j = [j for j in range(jlo, jhi + 1) if first[j] != o]
                    wk = [("ps", PO), ("ps", PD)]
                    rk = kvr + [("PT", pt), "ones"]

                    def mm2(c, rows_hi, stop_):
                        if rows_hi:
                            l_v, l_1, r_ = lw_hi, ones[64:128, 0:64], PT[pt][64:128, e, c]
                            tp = (64, e * 64)
                        else:
                            l_v, l_1, r_ = lw, ones[:, 0:64], PT[pt][:, e, c]
                            tp = (0, e * 64)
                        S.mm(ps(PO)[ol, c], l_v, r_, not started[e], stop_, rk, wk, skip=True, tp=tp)
                        S.mm(ps(PD)[ol, c], l_1, r_, not started[e], stop_, rk, wk, skip=True, tp=tp)
                        started[e] = True
                    if st_j:
                        if kind == "A" and o < 0:
                            j = st_j[0]
                            assert len(st_j) == 1
                            mm2(slice(j * 128, j * 128 + 64), False, False)
                            mm2(slice(j * 128 + 64, (j + 1) * 128), True, False)
                        else:
                            mm2(slice(st_j[0] * 128, (st_j[-1] + 1) * 128), False, last and not ct_j)
                    if ct_j:
                        mm2(slice(ct_j[0] * 128, (ct_j[-1] + 1) * 128), False, last)
                if last:
                    att = cfg[kind][5]
                    nk = "scrN"
                    if kind == "B":
                        S.act(scrN[:, :], ps(PD), AF.Ln, [("ps", PD), "esink"], [nk], bias=esink[:, m:m + 1], scale=1.0)
                    else:
                        S.act(scrN[:, :], ps(PD), AF.Ln, [("ps", PD)], [nk])
                    S.act(scrN[:, :], scrN[:, :], AF.Exp, [nk], [nk], scale=-1.0)
                    S.tt(att[:, m, :], ps(PO), scrN[:, :], ALU.mult, [("ps", PO), nk], ["att" + kind])

            LA = 2
            nA = sum(1 for (pi, o) in steps if pairs[pi]["kind"] == "A")
            nB = len(steps) - nA
            fa = list(fill_a)
            fb = list(fill_b)
            ea = max(1, nA // (len(fa) + 1)) if fa else 1
            eb = max(1, nB // (len(fb) + 1)) if fb else 1
            for n in range(len(steps) + LA):
                if n < len(steps):
                    if n == nA:
                        while fa:
                            fa.pop(0)()
                    qk(n)
                    if n < nA:
                        if fa and (n % ea == ea - 1):
                            fa.pop(0)()
                    else:
                        if fb and ((n - nA) % eb == eb - 1):
                            fb.pop(0)()
                if n >= LA:
                    pv(n - LA)
            while fa:
                fa.pop(0)()
            while fb:
                fb.pop(0)()

        def mixer_subtile(s, t, post_attn=None, stats=(), do_norm=True):
            half = t % 2
            kvk = ("kv", half)
            if do_norm:
                norm_apply(t, 1, lambda c: hs[:, c, :], "hs", t % 2)
            stats = list(stats)
            if stats:
                norm_sq(stats[0][0])
            slot = w_acquire()
            proj_fm(slot, 512, 0, 4, lambda oc: qaT[:, oc, :], lambda oc: "qTA")
            slot = w_acquire()
            proj_fm(slot, 512, 0, 4, lambda oc: kaT[:, oc, half * TS:(half + 1) * TS], lambda oc: kvk)
            if stats:
                norm_rest(*stats[0])
                if len(stats) > 1:
                    norm_sq(stats[1][0])
            slot = w_acquire()
            w = wsl[slot]
            for j in range(4):
                p = S.ps()
                for k in range(8):
                    S.mm(ps(p), hs[:, k, j * 128:(j + 1) * 128], w[:, k * 512:(k + 1) * 512], k == 0, k == 7,
                         [("w", slot), "hs"], [("ps", p)])
                S.copy(vA[:, half * 4 + j, :], ps(p), [("ps", p)], [kvk], eng=("act" if j % 2 else "dve"))

            st = {"slot": None, "fb": 0}

            def fbank():
                return S.ps()

            def f_proj(first, ncol_per_k, oc, out_ap, out_key):
                def f():
                    if first:
                        st["slot"] = w_acquire()
                    slot_ = st["slot"]
                    w_ = wsl[slot_]
                    p = fbank()
                    for k in range(8):
                        c0 = k * ncol_per_k + oc * 128
                        S.mm(ps(p), w_[:, c0:c0 + 128], hs[:, k, :], k == 0, k == 7, [("w", slot_), "hs"], [("ps", p)])
                    S.copy(out_ap, ps(p), [("ps", p)], [out_key], eng="act")
                return f

            def f_vb(j):
                def f():
                    slot_ = st["slot"]
                    w_ = wsl[slot_]
                    p = fbank()
                    for k in range(8):
                        S.mm(ps(p)[:, 0:128], hs[:, k, j * 128:(j + 1) * 128], w_[:, k * 384 + 256:k * 384 + 384],
                             k == 0, k == 7, [("w", slot_), "hs"], [("ps", p)])
                    S.copy(vB[:, half * 4 + j, :], ps(p)[:, 0:128], [("ps", p)], [kvk])
                return f

            def f_ga(d):
                def f():
                    if d % 4 == 0:
                        st["slot"] = w_acquire()
                    slot_ = st["slot"]
                    w_ = wsl[slot_]
                    p = fbank()
                    for k in range(8):
                        c0 = k * 512 + (d % 4) * 128
                        S.mm(ps(p), w_[:, c0:c0 + 128], hs[:, k, :], k == 0, k == 7, [("w", slot_), "hs"], [("ps", p)])
                    S.act(sgA[d], ps(p), AF.Tanh, [("ps", p)], [SGK[d]], scale=0.5)
                return f

            fill_a = [f_proj(oc == 0, 512, oc, qbT[:, oc, :], "qTB") for oc in range(4)]
            fill_a += [f_proj(oc == 0, 384, oc, kbT[:, oc, half * TS:(half + 1) * TS], kvk) for oc in range(2)]
            fill_a += [f_vb(j) for j in range(4)]
            fill_b = [f_ga(d) for d in range(8)]
            for f_ in fill_a:
                f_()
            if len(stats) > 1:
                norm_rest(*stats[1])
            for f_ in fill_b:
                f_()
            attention(t, [], [])

            for dp in range(4):
                slot = w_acquire()
                w = wsl[slot]
                for dd in range(2):
                    d = dp * 2 + dd
                    pya, pyb, pgb = S.ps(), S.ps(), S.ps()
                    for c in range(4):
                        c0 = 2048 + c * 256 + dd * 128
                        S.mm(ps(pya), w[:, c0:c0 + 128], attA[:, c, :], c == 0, c == 3,
                             [("w", slot), "attA"], [("ps", pya)])
                    for c in range(4):
                        c0 = 3072 + c * 256 + dd * 128
                        S.mm(ps(pyb), w[:, c0:c0 + 128], attB[:, c, :], c == 0, c == 3,
                             [("w", slot), "attB"], [("ps", pyb)])
                    for k in range(8):
                        c0 = k * 256 + dd * 128
                        S.mm(ps(pgb), w[:, c0:c0 + 128], hs[:, k, :], k == 0, k == 7,
                             [("w", slot), "hs"], [("ps", pgb)])
                    s0 = scrA[0][:, 0, :]
                    s1 = scrA[1][:, 0, :]
                    k0 = ("scr", 0, 0)
                    k1 = ("scr", 1, 0)
                    S.stt(s0, sgA[d], 1.0, ps(pya), ALU.add, ALU.mult, [SGK[d], ("ps", pya)], [k0])
                    S.act(s1, ps(pgb), AF.Tanh, [("ps", pgb)], [k1], scale=0.5)
                    S.stt(s1, s1, 1.0, ps(pyb), ALU.add, ALU.mult, [k1, ("ps", pyb)], [k1])
                    S.tt(merged[:, d, :], s0, s1, ALU.add, [k0, k1], [("mg", d)], eng="pool")
            if post_attn is not None:
                post_attn()
            for hf in range(2):
                slot = w_acquire()
                w = wsl[slot]
                for dd in range(4):
                    d2 = hf * 4 + dd
                    p = S.ps()
                    for d in range(8):
                        c0 = d * 512 + dd * 128
                        S.mm(ps(p), w[:, c0:c0 + 128], merged[:, d, :], d == 0, d == 7,
                             [("w", slot), ("mg", d)], [("ps", p)])
                    S.stt(xT[:, d2, tsl(t)], ps(p), 0.5, xT[:, d2, tsl(t)], ALU.mult, ALU.add,
                          [("ps", p), ("x", d2, t)], [("x", d2, t)])

        def hT_out(t):
            return lambda c: hT[:, c, tsl(t)]

        def prologue(g, t):
            if g == 0:
                todo = [0, 1] if t == 0 else ([t + 1] if t + 1 < NSUB else [])
                for tt in todo:
                    if tt > 0:
                        load_dma(0, tt, q="sp")
                    for jj in range(4):
                        load_tile(0, tt, jj)
                    norm_to(tt, 0, hT_out(tt), ("h", tt), tt)
                if t == 1:
                    late_setup()

        KVK = [("kv", 0), ("kv", 1)]
        LINK = [("lin", i_) for i_ in range(4)]
        load_dma(0, 0, q="sp")
        for s in range(SEQ_PER_CORE):
            S.alias(FFN_KEYS, MIX_KEYS)
            ffn_stage(s, 0, before_step=(prologue if s == 0 else None),
                      on_final=lambda t: (norm_rstd(0, 0) if t == 0 else None))
            S.alias(KVK, LINK)
            S.alias(MIX_KEYS, FFN_KEYS)
            FB = [2, 3, 0, 1]
            for t in range(NSUB):
                def nxt(t=t):
                    if t + 1 < NSUB:
                        norm_apply(t + 1, 1, lambda c: hs[:, c, :], "hs", (t + 1) % 2)

                st_ = []
                if t >= 1:
                    st_.append((t - 1, FB[t - 1]))
                if t + 1 < NSUB:
                    st_.append((t + 1, (t + 1) % 2))
                mixer_subtile(s, t, post_attn=nxt, stats=st_, do_norm=(t == 0))
            S.alias(FFN_KEYS, MIX_KEYS)
            S.alias(LINK, KVK)
            for t in range(NSUB):
                if t == NSUB - 1:
                    norm_rstd(NSUB - 1, FB[NSUB - 1])
                norm_apply(t, 2, hT_out(t), ("h", t), FB[t])
            if s + 1 < SEQ_PER_CORE:
                def fin(t, s=s):
                    store_subtile(s, t, after_tile=lambda jj: (load_tile(s + 1, t, jj - 1) if jj >= 1 else None))
                    load_tile(s + 1, t, 3)
                    if t >= 1:
                        norm_to(t - 1, 0, hT_out(t - 1), ("h", t - 1), t - 1)
                    if t + 1 < NSUB:
                        load_dma(s + 1, t + 1)

                def bstep(g, t, s=s):
                    if g == NG - 1 and t == 0:
                        load_dma(s + 1, 0)
            else:
                def fin(t, s=s):
                    store_subtile(s, t)
                bstep = None
            ffn_stage(s, 1, on_final=fin, before_step=bstep)
            if s + 1 < SEQ_PER_CORE:
                norm_to(NSUB - 1, 0, hT_out(NSUB - 1), ("h", NSUB - 1), NSUB - 1)

        n_out0 = S.dma_count["dma_out0"]
        n_out1 = S.dma_count["dma_out1"]
        with nc.Block() as block:
            def emit(name, e):
                for (eng, fn, waits, inc) in S.ops:
                    if eng != name:
                        continue
                    for semkey, val in waits:
                        e.wait_ge(sems[semkey], val)
                    fn(e).then_inc(sems[inc[0]], inc[1])

            @block.tensor
            def _(e):
                emit("pe", e)

            @block.scalar
            def _(e):
                emit("act", e)

            @block.vector
            def _(e):
                emit("dve", e)

            @block.gpsimd
            def _(e):
                emit("pool", e)

            @block.sync
            def _(e):
                emit("sp", e)
                e.wait_ge(sems["dma_out0"], n_out0)
                e.wait_ge(sems["dma_out1"], n_out1)
    return nc


_CACHE = {}


def kernel(x, ffn1_norm, ffn1_w_gate, ffn1_w_up, ffn1_w_down, mix_norm, w_in, rel_bias, sinks,
           w_proj_a, w_proj_b, w_out, ffn2_norm, ffn2_w_gate, ffn2_w_up, ffn2_w_down, final_norm):
    f = lambda a: np.asarray(a, dtype=np.float32)
    x = f(x)
    wf1 = _ffn_pieces(f(ffn1_w_gate), f(ffn1_w_up), f(ffn1_w_down))
    wf2 = _ffn_pieces(f(ffn2_w_gate), f(ffn2_w_up), f(ffn2_w_down))
    wmx = _mixer_pieces(f(w_in), f(w_proj_a), f(w_proj_b), f(w_out))
    gains = np.ascontiguousarray(np.concatenate([_gain(f(ffn1_norm)), _gain(f(mix_norm)), _gain(f(ffn2_norm))],
                                                axis=1))
    gfin = np.ascontiguousarray(np.broadcast_to(f(final_norm)[None, :], (128, D)))
    ta, ca, tb = _bias_tables(f(rel_bias))
    sk = f(sinks)
    snk = np.zeros((128, 4), np.float32)
    for m_ in range(4):
        snk[0:64, m_] = sk[2 * m_]
        snk[64:128, m_] = sk[2 * m_ + 1]
    if "nc" not in _CACHE:
        _CACHE["nc"] = build_program()
    nc = _CACHE["nc"]
    shared = {"wf1": wf1, "wf2": wf2, "wmx": wmx, "gains": gains, "gfin": gfin,
              "ta": np.ascontiguousarray(ta.reshape(128, -1)), "ca": ca,
              "tb": np.ascontiguousarray(tb.reshape(128, -1)), "snk": snk}
    in_maps = []
    for c in range(NCORES):
        m = dict(shared)
        m["x"] = np.ascontiguousarray(x[c * SEQ_PER_CORE:(c + 1) * SEQ_PER_CORE])
        in_maps.append(m)
    res = run_bass_kernel_spmd(nc, in_maps, core_ids=list(range(NCORES)))
    out = np.concatenate([np.asarray(r["y"], dtype=np.float32) for r in res.results], axis=0)
    return out
```

```python
import numpy as np
import concourse.bass as bass
import concourse.mybir as mybir
from concourse.bass_utils import run_bass_kernel_spmd

F32 = mybir.dt.float32
BF16 = mybir.dt.bfloat16
AF = mybir.ActivationFunctionType
ALU = mybir.AluOpType

NCORES = 8
D = 1024
SEQ = 2048
DFF = 2816
NG = 11
NSUB = 4
TS = 512
EPS = 1e-6
NEG = -30000.0
PIECE = 6144
NSLOT = 2
SEQ_PER_CORE = 2

ENGS = ("pe", "act", "dve", "pool", "sp")


class Sched:
    def __init__(self):
        self.ops = []
        self.count = {e: 0 for e in ENGS}
        self.dma_count = {}
        self.last_w = {}
        self.readers = {}
        self.seen = {e: {} for e in ENGS}
        self.ps_rr = 0

    def ps(self):
        i = self.ps_rr
        self.ps_rr = (self.ps_rr + 1) % 8
        return i

    def alias(self, dst, src):
        best = {}
        for k in src:
            refs = list(self.readers.get(k, ()))
            if k in self.last_w:
                refs.append(self.last_w[k])
            for (semkey, val, reng) in refs:
                if semkey not in best or best[semkey][1] < val:
                    best[semkey] = (semkey, val, reng)
        for d in dst:
            self.readers.setdefault(d, []).extend(best.values())

    def _add(self, eng, fn, reads, writes, dma=None):
        deps = {}

        def need(ref, raw):
            semkey, val, reng = ref
            if reng == eng and dma is None and not semkey.startswith("dma"):
                if eng == "pe":
                    return
            if deps.get(semkey, 0) < val:
                deps[semkey] = val

        for k in reads:
            w = self.last_w.get(k)
            if w is not None:
                need(w, True)
        for k in writes:
            w = self.last_w.get(k)
            if w is not None:
                need(w, False)
            for r in self.readers.get(k, ()):
                need(r, False)
        waits = []
        seen = self.seen[eng]
        for semkey, val in deps.items():
            if seen.get(semkey, 0) < val:
                seen[semkey] = val
                waits.append((semkey, val))
        if dma is not None:
            v = self.dma_count.get(dma, 0) + 16
            self.dma_count[dma] = v
            ref = (dma, v, eng)
            inc = (dma, 16)
        else:
            self.count[eng] += 1
            ref = (eng, self.count[eng], eng)
            inc = (eng, 1)
        for k in reads:
            self.readers.setdefault(k, []).append(ref)
        for k in writes:
            self.last_w[k] = ref
            self.readers[k] = []
        self.ops.append((eng, fn, waits, inc))
        return ref

    def mm(self, out, lhsT, rhs, start, stop, reads, writes, skip=False, tp=None):
        if skip:
            if tp is not None:
                return self._add("pe", lambda e: e.matmul(out, lhsT=lhsT, rhs=rhs, start=start, stop=stop,
                                                          skip_group_check=True, tile_position=tp), reads, writes)
            return self._add("pe", lambda e: e.matmul(out, lhsT=lhsT, rhs=rhs, start=start, stop=stop,
                                                      skip_group_check=True), reads, writes)
        return self._add("pe", lambda e: e.matmul(out, lhsT=lhsT, rhs=rhs, start=start, stop=stop),
                         reads, writes)

    def tr(self, out, in_, ident, reads, writes):
        return self._add("pe", lambda e: e.transpose(out, in_, ident), reads, writes)

    def act(self, out, in_, func, reads, writes, bias=None, scale=None):
        kw = {}
        if bias is not None:
            kw["bias"] = bias
        if scale is not None:
            kw["scale"] = scale
        return self._add("act", lambda e: e.activation(out=out, in_=in_, func=func, **kw), reads, writes)

    def tt(self, out, in0, in1, op, reads, writes, eng="dve"):
        return self._add(eng, lambda e: e.tensor_tensor(out=out, in0=in0, in1=in1, op=op), reads, writes)

    def stt(self, out, in0, scalar, in1, op0, op1, reads, writes):
        return self._add("dve", lambda e: e.scalar_tensor_tensor(out=out, in0=in0, scalar=scalar, in1=in1,
                                                                  op0=op0, op1=op1), reads, writes)

    def ts(self, out, in0, s1, op0, reads, writes, s2=None, op1=None):
        if op1 is None:
            return self._add("dve", lambda e: e.tensor_scalar(out=out, in0=in0, scalar1=s1, scalar2=None,
                                                              op0=op0), reads, writes)
        return self._add("dve", lambda e: e.tensor_scalar(out=out, in0=in0, scalar1=s1, scalar2=s2,
                                                          op0=op0, op1=op1), reads, writes)

    def copy(self, out, in_, reads, writes, eng="dve"):
        if eng == "act":
            return self._add("act", lambda e: e.copy(out=out, in_=in_), reads, writes)
        return self._add(eng, lambda e: e.tensor_copy(out=out, in_=in_), reads, writes)

    def recip(self, out, in_, reads, writes):
        return self._add("dve", lambda e: e.reciprocal(out=out, in_=in_), reads, writes)

    def memset(self, ap, val, writes, eng="dve"):
        return self._add(eng, lambda e: e.memset(ap, val), (), writes)

    def dma(self, queue, out, in_, semkey, reads, writes):
        return self._add(queue, lambda e: e.dma_start(out=out, in_=in_), reads, writes, dma=semkey)

    def raw(self, eng, fn, reads, writes):
        return self._add(eng, fn, reads, writes)


def _kmajor(w):
    K = w.shape[0] // 128
    return np.ascontiguousarray(w.reshape(K, 128, w.shape[1]).transpose(1, 0, 2)).reshape(128, -1)


def _ffn_pieces(wg, wu, wd):
    out = np.zeros((NG, 128, PIECE), np.float32)
    for g in range(NG):
        out[g, :, 0:2048] = _kmajor(wg[:, g * 256:(g + 1) * 256])
        out[g, :, 2048:4096] = _kmajor(wu[:, g * 256:(g + 1) * 256])
        out[g, :, 4096:6144] = _kmajor(wd[g * 256:(g + 1) * 256, :])
    return out


def _mixer_pieces(w_in, wpa, wpb, wo):
    out = np.zeros((NMIX, 128, PIECE), np.float32)
    out[0, :, 0:4096] = _kmajor(w_in[:, 0:512])
    out[1, :, 0:4096] = _kmajor(w_in[:, 512:1024])
    out[2, :, 0:4096] = _kmajor(w_in[:, 1024:1536])
    out[3, :, 0:4096] = _kmajor(w_in[:, 1536:2048])
    kb = w_in[:, 2048:2176]
    vb = w_in[:, 2176:2304]
    kvb = np.concatenate([kb[:, 0:64], kb[:, 0:64], kb[:, 64:128], kb[:, 64:128], vb], axis=1)
    out[4, :, 0:3072] = _kmajor(kvb)
    out[5, :, 0:4096] = _kmajor(w_in[:, 2304:2816])
    out[6, :, 0:4096] = _kmajor(w_in[:, 2816:3328])
    for dp in range(4):
        sl = slice(dp * 256, (dp + 1) * 256)
        out[7 + dp, :, 0:2048] = _kmajor(w_in[:, 3328:4352][:, sl])
        out[7 + dp, :, 2048:3072] = _kmajor(wpa[:, sl])
        out[7 + dp, :, 3072:4096] = _kmajor(wpb[:, sl])
    out[11, :, 0:4096] = _kmajor(wo[:, 0:512])
    out[12, :, 0:4096] = _kmajor(wo[:, 512:1024])
    return out


NMIX = 13
MIX_N = [4096, 4096, 4096, 4096, 3072, 4096, 4096, 4096, 4096, 4096, 4096, 4096, 4096]


def _bias_tables(rel_bias):
    p = np.arange(128)[:, None]
    q = np.arange(128)[None, :]
    cq = q // 64
    cp = p // 64
    ta = np.zeros((128, 8, 2, 128), np.float32)
    for r, s in enumerate((4, 3)):
        rel = q - p + (4 - s) * 128
        dc = cq + 8 - 2 * s - cp
        valid = (dc >= 0) & (dc <= 8)
        idx = np.clip(rel, -63, 128) + 63
        for h in range(8):
            ta[:, h, r, :] = np.where(valid, rel_bias[h][idx], np.float32(NEG))
    ca = np.ascontiguousarray(np.broadcast_to(rel_bias[:, 191][None, :], (128, 8))).astype(np.float32)
    tb = np.zeros((128, 8, 2, 128), np.float32)
    for r, s in enumerate((1, 0)):
        dist = np.abs(q - p + (1 - s) * 128).astype(np.float32)
        dc = cq + 2 - 2 * s - cp
        valid = (dc >= 0) & (dc <= 2)
        for h in range(8):
            slope = np.float32(2.0 ** (-8.0 * (h + 1) / 8))
            tb[:, h, r, :] = np.where(valid, -slope * dist, np.float32(NEG))
    return ta, ca, tb


def _gain(g):
    return np.ascontiguousarray(g.reshape(8, 128).T).astype(np.float32)


def build_program():
    nc = bass.Bass("TRN2", target_bir_lowering=False)
    x_d = nc.dram_tensor("x", [SEQ_PER_CORE, SEQ, D], F32, kind="ExternalInput").ap()
    y_d = nc.dram_tensor("y", [SEQ_PER_CORE, SEQ, D], F32, kind="ExternalOutput").ap()
    wf1_d = nc.dram_tensor("wf1", [NG, 128, PIECE], F32, kind="ExternalInput").ap()
    wf2_d = nc.dram_tensor("wf2", [NG, 128, PIECE], F32, kind="ExternalInput").ap()
    wmx_d = nc.dram_tensor("wmx", [NMIX, 128, PIECE], F32, kind="ExternalInput").ap()
    gains_d = nc.dram_tensor("gains", [128, 3 * 8], F32, kind="ExternalInput").ap()
    gfin_d = nc.dram_tensor("gfin", [128, D], F32, kind="ExternalInput").ap()
    ta_d = nc.dram_tensor("ta", [128, 8 * 2 * 128], F32, kind="ExternalInput").ap()
    ca_d = nc.dram_tensor("ca", [128, 8], F32, kind="ExternalInput").ap()
    tb_d = nc.dram_tensor("tb", [128, 8 * 2 * 128], F32, kind="ExternalInput").ap()
    snk_d = nc.dram_tensor("snk", [128, 4], F32, kind="ExternalInput").ap()

    wmb_d = nc.dram_tensor("wmx_bf16", [NMIX, 128, PIECE], BF16, kind="Internal").ap()

    S = Sched()
    from contextlib import ExitStack
    with ExitStack() as es:
        def sb(name, shape, dt):
            return es.enter_context(nc.sbuf_tensor(name, shape, dt))

        xT = sb("xT", [128, 8, SEQ], F32)
        wsl = [sb("wsl%d" % i, [128, PIECE], BF16) for i in range(NSLOT)]
        ident = sb("ident", [128, 128], F32)
        ones = sb("ones", [128, 128], BF16)
        mhalf_t = sb("mhalf", [128, 1], F32)
        mhalf = mhalf_t[:, 0:1].to_broadcast([128, TS])
        gains = sb("gains_s", [128, 24], F32)
        gfin = sb("gfin_s", [128, D], F32)
        ta = sb("ta_s", [128, 8, 2, 128], F32)
        ca = sb("ca_s", [128, 8], F32)
        tb = sb("tb_s", [128, 8, 2, 128], F32)
        esink = sb("esink", [128, 4], F32)
        ssq = sb("ssq", [128, 8], F32)
        R = sb("R", [128, 8 * SEQ], BF16)
        hT = R[:].rearrange("p (c n) -> p c n", c=8)
        XS = sb("XS", [128, 2 * D], F32)
        xin = [XS[:, i * D:(i + 1) * D] for i in range(2)]
        sq_t = sb("sq", [128, 8 * TS], BF16)
        sq = sq_t[:].rearrange("p (c n) -> p c n", c=8)
        lin = None
        sgA = [XS[:, d_ * TS:(d_ + 1) * TS] for d_ in range(4)] + \
              [sq_t[:].bitcast(F32)[:, d_ * TS:(d_ + 1) * TS] for d_ in range(4)]
        rstd = [sb("rstd%d" % i, [128, TS], F32) for i in range(4)]
        aT = [sb("aT%d" % i, [128, 2, TS], BF16) for i in range(2)]
        scrA = [sb("scrA%d" % i, [128, 2, TS], F32) for i in range(2)]
        scrN = sb("scrN", [128, TS], F32)
        hs = R[:, 0:4096].rearrange("p (c n) -> p c n", c=8)
        qaT = R[:, 4096:6144].rearrange("p (c n) -> p c n", c=4)
        qbT = R[:, 6144:8192].rearrange("p (c n) -> p c n", c=4)
        attA = R[:, 8192:10240].rearrange("p (c n) -> p c n", c=4)
        attB = R[:, 10240:12288].rearrange("p (c n) -> p c n", c=4)
        merged = R[:, 12288:16384].rearrange("p (c n) -> p c n", c=8)
        kaT_t = sb("kaT", [128, 4 * 2 * TS], BF16)
        kaT = kaT_t[:].rearrange("p (c n) -> p c n", c=4)
        lin = [kaT_t[:].bitcast(F32)[:, i * D:(i + 1) * D] for i in range(2)]
        kbT = sb("kbT", [128, 2, 2 * TS], BF16)
        vA_t = sb("vA", [128, 8 * 512], BF16)
        vA = vA_t[:].rearrange("p (a c) -> p a c", a=8)
        lin = lin + [vA_t[:].bitcast(F32)[:, i * D:(i + 1) * D] for i in range(2)]
        vB = sb("vB", [128, 8, 128], BF16)
        NPT = 3
        PT = [sb("PT%d" % i, [128, 2, TS], BF16) for i in range(NPT)]
        pp = [es.enter_context(nc.psum_tensor("pp%d" % i, [128, 2, 512], F32)) for i in range(4)]

        def ps(i):
            return pp[i // 2][:, i % 2, :]

        sem_names = list(ENGS) + ["dma_w%d" % i for i in range(NSLOT)] + \
            ["dma_xin0", "dma_xin1", "dma_xin2", "dma_xin3", "dma_xs2", "dma_xs3", "dma_out0", "dma_out1"] + ["dma_c%d" % i for i in range(6)] + \
            ["dma_cv%d" % i for i in range(NMIX)] + ["dma_wm%d" % i for i in range(NSLOT)] + ["dma_xs0", "dma_xs1"]
        sems = {n: es.enter_context(nc.semaphore(n)) for n in sem_names}

        FFN_KEYS = [("h", t_) for t_ in range(NSUB)]
        MIX_KEYS = ["hs", "qaT", "qbT", "attA", "attB"] + [("mg", d_) for d_ in range(8)]
        SQK = [("sq", 0), ("sq", 1)]
        SGK = [("xin", 0), ("xin", 0), ("xin", 1), ("xin", 1), ("sq", 0), ("sq", 0), ("sq", 1), ("sq", 1)]

        S.dma("sp", gains[:], gains_d, "dma_c0", (), ["gains"])
        S.memset(ones[:], 1.0, ["ones"])
        S.memset(mhalf_t[:], -0.5, ["mhalf"])
        S.memset(ident[:], 0.0, ["ident"], eng="pool")
        S.raw("pool", lambda e: e.affine_select(out=ident[:], in_=ident[:], pattern=[[-1, 128]],
                                                compare_op=ALU.not_equal, fill=1.0, base=0,
                                                channel_multiplier=1), ["ident"], ["ident"])
        def late_setup():
            S.dma("sp", ta[:].rearrange("p a b c -> p (a b c)"), ta_d, "dma_c1", (), ["ta"])
            S.dma("sp", ca[:], ca_d, "dma_c2", (), ["ca"])
            S.dma("sp", tb[:].rearrange("p a b c -> p (a b c)"), tb_d, "dma_c3", (), ["tb"])
            S.dma("sp", esink[:], snk_d, "dma_c4", (), ["esink"])
            S.dma("sp", gfin[:], gfin_d, "dma_c5", (), ["gfin"])
            S.act(esink[:], esink[:], AF.Exp, ["esink"], ["esink"])
            for h in range(8):
                S.ts(ta[:, h, :, :].rearrange("p a b -> p (a b)"), ta[:, h, :, :].rearrange("p a b -> p (a b)"),
                     ca[:, h:h + 1], ALU.subtract, ["ta", "ca"], ["ta"])

        wstate = {"n": 0, "issued": 0, "sched": []}

        def wsched_build():
            sched = []
            for _ in range(SEQ_PER_CORE):
                sched += [(wf1_d, g, PIECE, "f") for g in range(NG)]
                for _t in range(NSUB):
                    sched += [(wmb_d, i, MIX_N[i], "m") for i in range(NMIX)]
                sched += [(wf2_d, g, PIECE, "f") for g in range(NG)]
            return sched
        wstate["sched"] = wsched_build()
        wstate["conv"] = 0

        def w_issue(upto):
            while wstate["issued"] <= upto and wstate["issued"] < len(wstate["sched"]):
                i = wstate["issued"]
                src, idx, n, kd = wstate["sched"][i]
                slot = i % NSLOT
                if kd == "f":
                    S.dma("pool", wsl[slot][:, 0:n], src[idx, :, 0:n], "dma_w%d" % slot,
                          ([("lin", j_) for j_ in range(4)] if i == 0 else ()), [("w", slot)])
                    for _r in range(0 if i < 2 else 2):
                        if wstate["conv"] >= NMIX:
                            break
                        c = wstate["conv"]
                        S.dma("pool", wmb_d[c, :, 0:MIX_N[c]], wmx_d[c, :, 0:MIX_N[c]], "dma_cv%d" % c, (),
                              [("wbf", c)])
                        wstate["conv"] += 1
                else:
                    while wstate["conv"] < NMIX:
                        c = wstate["conv"]
                        S.dma("pool", wmb_d[c, :, 0:MIX_N[c]], wmx_d[c, :, 0:MIX_N[c]], "dma_cv%d" % c, (),
                              [("wbf", c)])
                        wstate["conv"] += 1
                    assert ("wbf", idx) in S.last_w
                    S.dma("sp", wsl[slot][:, 0:n], src[idx, :, 0:n], "dma_wm%d" % slot, [("wbf", idx)],
                          [("w", slot)])
                wstate["issued"] += 1

        def w_acquire(prefetch=True):
            i = wstate["n"]
            wstate["n"] += 1
            w_issue(i + NSLOT - 1 if prefetch else i)
            return i % NSLOT

        def w_prefetch():
            w_issue(wstate["n"] - 1 + NSLOT - 1)

        def tsl(t):
            return slice(t * TS, (t + 1) * TS)

        def norm_sq(t):
            xk = [("x", c, t) for c in range(8)]
            S.act(sq, xT[:, :, tsl(t)], AF.Square, xk, SQK)

        def norm_rest(t, rb):
            p = S.ps()
            for c in range(8):
                S.mm(ps(p), ones[:], sq[:, c, :], c == 0, c == 7, ["ones"] + SQK, [("ps", p)])
            S.act(rstd[rb][:], ps(p), AF.Sqrt, [("ps", p)], [("rstd", rb)], bias=EPS, scale=1.0 / D)
            S.recip(rstd[rb][:], rstd[rb][:], [("rstd", rb)], [("rstd", rb)])

        def norm_rstd(t, rb):
            norm_sq(t)
            norm_rest(t, rb)

        def norm_to(t, gidx, out_fn, out_key, rb):
            norm_rstd(t, rb)
            norm_apply(t, gidx, out_fn, out_key, rb)

        def norm_apply(t, gidx, out_fn, out_key, rb):
            for c in range(8):
                S.stt(out_fn(c), xT[:, c, tsl(t)], gains[:, gidx * 8 + c:gidx * 8 + c + 1], rstd[rb][:],
                      ALU.mult, ALU.mult, [("x", c, t), "gains", ("rstd", rb)], [out_key])

        def load_dma(s, t, q="pool"):
            for jj in range(4):
                i = t * 4 + jj
                S.dma(q, lin[jj], x_d[s, i * 128:(i + 1) * 128, :], ("dma_xin%d" if q == "pool" else "dma_xs%d") % jj,
                      (), [("lin", jj)])

        def load_tile(s, t, jj):
            i = t * 4 + jj
            for hf in range(2):
                p = S.ps()
                for cc in range(4):
                    c = hf * 4 + cc
                    S.tr(ps(p)[:, cc * 128:(cc + 1) * 128], lin[jj][:, c * 128:(c + 1) * 128], ident[:],
                         [("lin", jj), "ident"], [("ps", p)])
                S.copy(xT[:, hf * 4:hf * 4 + 4, i * 128:(i + 1) * 128],
                       ps(p).rearrange("p (c n) -> p c n", c=4),
                       [("ps", p)], [("x", hf * 4 + cc, t) for cc in range(4)],
                       eng=("act" if hf == 0 else "dve"))

        def load_subtile(s, t, q="pool"):
            load_dma(s, t, q)
            for jj in range(4):
                load_tile(s, t, jj)

        def store_subtile(s, t, after_tile=None):
            for jj in range(4):
                store_tile(s, t, jj)
                if after_tile is not None:
                    after_tile(jj)

        def store_tile(s, t, jj):
            if True:
                i = t * 4 + jj
                b = i % 2
                pb = []
                for hf in range(2):
                    p = S.ps()
                    pb.append(p)
                    for cc in range(4):
                        c = hf * 4 + cc
                        S.tr(ps(p)[:, cc * 128:(cc + 1) * 128], xT[:, c, i * 128:(i + 1) * 128], ident[:],
                             [("x", c, t), "ident"], [("ps", p)])
                    S.raw("act", (lambda e, p=p, hf=hf, b=b: e.activation(
                        out=scrN[:, :].bitcast(BF16)[:, 0:TS], in_=ps(p), func=AF.Square,
                        accum_out=ssq[:, b * 4 + hf:b * 4 + hf + 1])), [("ps", p)], ["scrN", ("ssq", b, hf)])
                S.tt(ssq[:, b * 4 + 2:b * 4 + 3], ssq[:, b * 4:b * 4 + 1], ssq[:, b * 4 + 1:b * 4 + 2], ALU.add,
                     [("ssq", b, 0), ("ssq", b, 1)], [("ssq", b, 2)])
                S.ts(ssq[:, b * 4 + 2:b * 4 + 3], ssq[:, b * 4 + 2:b * 4 + 3], 1.0 / D, ALU.mult,
                     [("ssq", b, 2)], [("ssq", b, 2)], s2=EPS, op1=ALU.add)
                S.tt(ssq[:, b * 4 + 3:b * 4 + 4], ssq[:, b * 4 + 2:b * 4 + 3], mhalf_t[:, 0:1], ALU.pow,
                     [("ssq", b, 2), "mhalf"], [("ssq", b, 3)], eng="pool")
                for hf in range(2):
                    S.stt(xin[b][:, hf * 512:(hf + 1) * 512], ps(pb[hf]), ssq[:, b * 4 + 3:b * 4 + 4],
                          gfin[:, hf * 512:(hf + 1) * 512], ALU.mult, ALU.mult,
                          [("ps", pb[hf]), ("ssq", b, 3), "gfin"], [("xin", b)])
                S.dma("sp", y_d[s, i * 128:(i + 1) * 128, :], xin[b], "dma_out%d" % b, [("xin", b)], [])

        def ffn_stage(s, which, on_final=None, before_step=None):
            steps = [(g, t) for g in range(NG) for t in range(NSUB)]
            pend = None
            slot = None

            def flush(pend):
                g_, t_, slot_, ab_ = pend
                down(t_, slot_, ab_)
                if g_ == NG - 1 and on_final is not None:
                    on_final(t_)

            for si, (g, t) in enumerate(steps):
                if t == 0:
                    slot = w_acquire(prefetch=(pend is None))
                    need_pf = pend is not None
                ab = si % 2
                w = wsl[slot]
                if before_step is not None:
                    before_step(g, t)
                for fc in range(2):
                    pg = S.ps()
                    pu = S.ps()
                    for k in range(8):
                        S.mm(ps(pg), w[:, k * 256 + fc * 128:k * 256 + fc * 128 + 128], hT[:, k, tsl(t)],
                             k == 0, k == 7, [("w", slot), ("h", t)], [("ps", pg)])
                    for k in range(8):
                        S.mm(ps(pu), w[:, 2048 + k * 256 + fc * 128:2048 + k * 256 + fc * 128 + 128],
                             hT[:, k, tsl(t)], k == 0, k == 7, [("w", slot), ("h", t)], [("ps", pu)])
                    S.act(scrA[fc][:, 0, :], ps(pg), AF.Silu, [("ps", pg)], [("scr", fc, 0)])
                    S.tt(aT[ab][:, fc, :], scrA[fc][:, 0, :], ps(pu), ALU.mult,
                         [("scr", fc, 0), ("ps", pu)], [("aT", ab, fc)])
                if pend is not None:
                    flush(pend)
                if t == 0 and need_pf:
                    w_prefetch()
                pend = (g, t, slot, ab)
            flush(pend)

        def down(t, slot, ab):
            w = wsl[slot]
            for d in range(8):
                p = S.ps()
                for fc in range(2):
                    S.mm(ps(p), w[:, 4096 + fc * 1024 + d * 128:4096 + fc * 1024 + d * 128 + 128],
                         aT[ab][:, fc, :], fc == 0, fc == 1, [("w", slot), ("aT", ab, fc)], [("ps", p)])
                S.stt(xT[:, d, tsl(t)], ps(p), 0.5, xT[:, d, tsl(t)], ALU.mult, ALU.add,
                      [("ps", p), ("x", d, t)], [("x", d, t)])

        def proj_fm(slot, ncol_per_k, col0, n_oc, out_fn, out_keys):
            w = wsl[slot]
            for oc in range(n_oc):
                p = S.ps()
                for k in range(8):
                    c0 = k * ncol_per_k + col0 + oc * 128
                    S.mm(ps(p), w[:, c0:c0 + 128], hs[:, k, :], k == 0, k == 7,
                         [("w", slot), "hs"], [("ps", p)])
                S.copy(out_fn(oc), ps(p), [("ps", p)], [out_keys(oc)], eng="act")

        def attention(t, fill_a, fill_b):
            kvr = [("kv", 0), ("kv", 1)]
            cfg = {"A": (-4, 5, ta, kaT, qaT, attA), "B": (-1, 2, tb, kbT, qbT, attB)}
            steps = []
            pairs = []
            for kind in ("A", "B"):
                omin, width = cfg[kind][0], cfg[kind][1]
                for m in range(4):
                    os_ = [o for o in range(omin, 4) if 4 * t + o >= 0]
                    first = {}
                    for o in os_:
                        for j in range(max(0, o), min(3, o + width - 1) + 1):
                            first.setdefault(j, o)
                    pi = len(pairs)
                    pairs.append({"kind": kind, "m": m, "first": first, "started": [False, False],
                                  "PO": 4 + 2 * (pi % 2), "PD": 5 + 2 * (pi % 2), "last": os_[-1]})
                    for o in os_:
                        steps.append((pi, o))
            info = {}

            def qk(n):
                pi, o = steps[n]
                P = pairs[pi]
                kind, m = P["kind"], P["m"]
                omin, width, tab, kT, qT, att = cfg[kind]
                sp_ = n % 2
                pt = n % NPT
                sa = n % 2
                pos = (4 * t + o) % 8
                jlo, jhi = max(0, o), min(3, o + width - 1)
                cs = slice(jlo * 128, (jhi + 1) * 128)
                for e in range(2):
                    kc = m if kind == "A" else m // 2
                    S.mm(pp[sp_][:, e, cs], kT[e * 64:(e + 1) * 64, kc, pos * 128:(pos + 1) * 128],
                         qT[e * 64:(e + 1) * 64, m, cs], True, True,
                         kvr + ["qT" + kind], [("ps", 2 * sp_ + e)])
                pkeys = [("ps", 2 * sp_), ("ps", 2 * sp_ + 1)]
                if kind == "B":
                    tj = list(range(jlo, jhi + 1))
                else:
                    tj = [j for j in (o, o + 1) if jlo <= j <= jhi]
                dj = [j for j in range(jlo, jhi + 1) if j not in tj]
                if tj:
                    r0 = tj[0] - o
                    r1 = tj[-1] - o
                    c = slice(tj[0] * 128, (tj[-1] + 1) * 128)
                    S.stt(scrA[sa][:, :, c], pp[sp_][:, :, c], 0.125,
                          tab[:, 2 * m:2 * m + 2, r0:r1 + 1, :].rearrange("p h r q -> p h (r q)"),
                          ALU.mult, ALU.add, pkeys + ["t" + kind.lower()], [("scr", sa, 0), ("scr", sa, 1)])
                    S.act(PT[pt][:, :, c], scrA[sa][:, :, c], AF.Exp, [("scr", sa, 0), ("scr", sa, 1)],
                          [("PT", pt)])
                if dj:
                    c = slice(dj[0] * 128, (dj[-1] + 1) * 128)
                    S.act(PT[pt][:, :, c], pp[sp_][:, :, c], AF.Exp, pkeys, [("PT", pt)], scale=0.125)
                info[n] = (pt, pos, jlo, jhi)

            def pv(n):
                pi, o = steps[n]
                P = pairs[pi]
                kind, m, first, started = P["kind"], P["m"], P["first"], P["started"]
                PO, PD = P["PO"], P["PD"]
                pt, pos, jlo, jhi = info[n]
                last = (o == P["last"])
                for e in range(2):
                    h = 2 * m + e
                    ol = slice(e * 64, (e + 1) * 64)
                    if kind == "A":
                        vcols = slice(h * 64, (h + 1) * 64)
                        lw, lw_hi = vA[:, pos, vcols], vA[64:128, pos, vcols]
                    else:
                        g = m // 2
                        vcols = slice(g * 64, (g + 1) * 64)
                        lw, lw_hi = vB[:, pos, vcols], None
                    st_j = [j for j in range(jlo, jhi + 1) if first[j] == o]
                    ct_j = [j for j in range(jlo, jhi + 1) if first[j] != o]
                    wk = [("ps", PO), ("ps", PD)]
                    rk = kvr + [("PT", pt), "ones"]

                    def mm2(c, rows_hi, stop_):
                        if rows_hi:
                            l_v, l_1, r_ = lw_hi, ones[64:128, 0:64], PT[pt][64:128, e, c]
                            tp = (64, e * 64)
                        else:
                            l_v, l_1, r_ = lw, ones[:, 0:64], PT[pt][:, e, c]
                            tp = (0, e * 64)
                        S.mm(ps(PO)[ol, c], l_v, r_, not started[e], stop_, rk, wk, skip=True, tp=tp)
                        S.mm(ps(PD)[ol, c], l_1, r_, not started[e], stop_, rk, wk, skip=True, tp=tp)
                        started[e] = True
                    if kind == "B" and st_j and ct_j:
                        mm2(slice(jlo * 128, (jhi + 1) * 128), False, last)
                    elif st_j:
                        if kind == "A" and o < 0:
                            j = st_j[0]
                            assert len(st_j) == 1
                            mm2(slice(j * 128, j * 128 + 64), False, False)
                            mm2(slice(j * 128 + 64, (j + 1) * 128), True, False)
                        else:
                            mm2(slice(st_j[0] * 128, (st_j[-1] + 1) * 128), False, last and not ct_j)
                    if ct_j and not (kind == "B" and st_j):
                        mm2(slice(ct_j[0] * 128, (ct_j[-1] + 1) * 128), False, last)
                if last:
                    att = cfg[kind][5]
                    nk = "scrN"
                    if kind == "B":
                        S.act(scrN[:, :], ps(PD), AF.Ln, [("ps", PD), "esink"], [nk], bias=esink[:, m:m + 1], scale=1.0)
                    else:
                        S.act(scrN[:, :], ps(PD), AF.Ln, [("ps", PD)], [nk])
                    S.act(scrN[:, :], scrN[:, :], AF.Exp, [nk], [nk], scale=-1.0)
                    S.tt(att[:, m, :], ps(PO), scrN[:, :], ALU.mult, [("ps", PO), nk], ["att" + kind])

            LA = 2
            nA = sum(1 for (pi, o) in steps if pairs[pi]["kind"] == "A")
            nB = len(steps) - nA
            fa = list(fill_a)
            fb = list(fill_b)
            ea = max(1, nA // (len(fa) + 1)) if fa else 1
            eb = max(1, nB // (len(fb) + 1)) if fb else 1
            for n in range(len(steps) + LA):
                if n < len(steps):
                    if n == nA:
                        while fa:
                            fa.pop(0)()
                    qk(n)
                    if n < nA:
                        if fa and (n % ea == ea - 1):
                            fa.pop(0)()
                    else:
                        if fb and ((n - nA) % eb == eb - 1):
                            fb.pop(0)()
                if n >= LA:
                    pv(n - LA)
            while fa:
                fa.pop(0)()
            while fb:
                fb.pop(0)()

        def mixer_subtile(s, t, post_attn=None, stats=(), do_norm=True):
            half = t % 2
            kvk = ("kv", half)
            if do_norm:
                norm_apply(t, 1, lambda c: hs[:, c, :], "hs", t % 2)
            stats = list(stats)
            if stats:
                norm_sq(stats[0][0])
            slot = w_acquire()
            proj_fm(slot, 512, 0, 4, lambda oc: qaT[:, oc, :], lambda oc: "qTA")
            slot = w_acquire()
            proj_fm(slot, 512, 0, 4, lambda oc: kaT[:, oc, half * TS:(half + 1) * TS], lambda oc: kvk)
            if stats:
                norm_rest(*stats[0])
                if len(stats) > 1:
                    norm_sq(stats[1][0])
            slot = w_acquire()
            w = wsl[slot]
            for j in range(4):
                p = S.ps()
                for k in range(8):
                    S.mm(ps(p), hs[:, k, j * 128:(j + 1) * 128], w[:, k * 512:(k + 1) * 512], k == 0, k == 7,
                         [("w", slot), "hs"], [("ps", p)])
                S.copy(vA[:, half * 4 + j, :], ps(p), [("ps", p)], [kvk], eng=("act" if j % 2 else "dve"))

            st = {"slot": None, "fb": 0}

            def fbank():
                return S.ps()

            def f_proj(first, ncol_per_k, oc, out_ap, out_key):
                def f():
                    if first:
                        st["slot"] = w_acquire()
                    slot_ = st["slot"]
                    w_ = wsl[slot_]
                    p = fbank()
                    for k in range(8):
                        c0 = k * ncol_per_k + oc * 128
                        S.mm(ps(p), w_[:, c0:c0 + 128], hs[:, k, :], k == 0, k == 7, [("w", slot_), "hs"], [("ps", p)])
                    S.copy(out_ap, ps(p), [("ps", p)], [out_key], eng="act")
                return f

            def f_vb(j):
                def f():
                    slot_ = st["slot"]
                    w_ = wsl[slot_]
                    p = fbank()
                    for k in range(8):
                        S.mm(ps(p)[:, 0:128], hs[:, k, j * 128:(j + 1) * 128], w_[:, k * 384 + 256:k * 384 + 384],
                             k == 0, k == 7, [("w", slot_), "hs"], [("ps", p)])
                    S.copy(vB[:, half * 4 + j, :], ps(p)[:, 0:128], [("ps", p)], [kvk])
                return f

            def f_ga(d):
                def f():
                    if d % 4 == 0:
                        st["slot"] = w_acquire()
                    slot_ = st["slot"]
                    w_ = wsl[slot_]
                    p = fbank()
                    for k in range(8):
                        c0 = k * 512 + (d % 4) * 128
                        S.mm(ps(p), w_[:, c0:c0 + 128], hs[:, k, :], k == 0, k == 7, [("w", slot_), "hs"], [("ps", p)])
                    S.act(sgA[d], ps(p), AF.Tanh, [("ps", p)], [SGK[d]], scale=0.5)
                return f

            fill_a = [f_proj(oc == 0, 512, oc, qbT[:, oc, :], "qTB") for oc in range(4)]
            fill_a += [f_proj(oc == 0, 384, oc, kbT[:, oc, half * TS:(half + 1) * TS], kvk) for oc in range(2)]
            fill_a += [f_vb(j) for j in range(4)]
            fill_b = [f_ga(d) for d in range(8)]
            for f_ in fill_a:
                f_()
            if len(stats) > 1:
                norm_rest(*stats[1])
            for f_ in fill_b:
                f_()
            attention(t, [], [])

            for dp in range(4):
                slot = w_acquire()
                w = wsl[slot]
                for dd in range(2):
                    d = dp * 2 + dd
                    pya, pyb, pgb = S.ps(), S.ps(), S.ps()
                    for c in range(4):
                        c0 = 2048 + c * 256 + dd * 128
                        S.mm(ps(pya), w[:, c0:c0 + 128], attA[:, c, :], c == 0, c == 3,
                             [("w", slot), "attA"], [("ps", pya)])
                    for c in range(4):
                        c0 = 3072 + c * 256 + dd * 128
                        S.mm(ps(pyb), w[:, c0:c0 + 128], attB[:, c, :], c == 0, c == 3,
                             [("w", slot), "attB"], [("ps", pyb)])
                    for k in range(8):
                        c0 = k * 256 + dd * 128
                        S.mm(ps(pgb), w[:, c0:c0 + 128], hs[:, k, :], k == 0, k == 7,
                             [("w", slot), "hs"], [("ps", pgb)])
                    s0 = scrA[0][:, 0, :]
                    s1 = scrA[1][:, 0, :]
                    k0 = ("scr", 0, 0)
                    k1 = ("scr", 1, 0)
                    S.stt(s0, sgA[d], 1.0, ps(pya), ALU.add, ALU.mult, [SGK[d], ("ps", pya)], [k0])
                    S.act(s1, ps(pgb), AF.Tanh, [("ps", pgb)], [k1], scale=0.5)
                    S.stt(s1, s1, 1.0, ps(pyb), ALU.add, ALU.mult, [k1, ("ps", pyb)], [k1])
                    S.tt(merged[:, d, :], s0, s1, ALU.add, [k0, k1], [("mg", d)], eng="pool")
            if post_attn is not None:
                post_attn()
            for hf in range(2):
                slot = w_acquire()
                w = wsl[slot]
                for dd in range(4):
                    d2 = hf * 4 + dd
                    p = S.ps()
                    for d in range(8):
                        c0 = d * 512 + dd * 128
                        S.mm(ps(p), w[:, c0:c0 + 128], merged[:, d, :], d == 0, d == 7,
                             [("w", slot), ("mg", d)], [("ps", p)])
                    S.stt(xT[:, d2, tsl(t)], ps(p), 0.5, xT[:, d2, tsl(t)], ALU.mult, ALU.add,
                          [("ps", p), ("x", d2, t)], [("x", d2, t)])

        def hT_out(t):
            return lambda c: hT[:, c, tsl(t)]

        def prologue(g, t):
            if g == 0:
                todo = [0, 1] if t == 0 else ([t + 1] if t + 1 < NSUB else [])
                for tt in todo:
                    if tt > 0:
                        load_dma(0, tt, q="sp")
                    for jj in range(4):
                        load_tile(0, tt, jj)
                    norm_to(tt, 0, hT_out(tt), ("h", tt), tt)
                if t == 1:
                    late_setup()

        KVK = [("kv", 0), ("kv", 1)]
        LINK = [("lin", i_) for i_ in range(4)]
        load_dma(0, 0, q="sp")
        for s in range(SEQ_PER_CORE):
            S.alias(FFN_KEYS, MIX_KEYS)
            ffn_stage(s, 0, before_step=(prologue if s == 0 else None),
                      on_final=lambda t: (norm_rstd(0, 0) if t == 0 else None))
            S.alias(KVK, LINK)
            S.alias(MIX_KEYS, FFN_KEYS)
            FB = [2, 3, 0, 1]
            for t in range(NSUB):
                def nxt(t=t):
                    if t + 1 < NSUB:
                        norm_apply(t + 1, 1, lambda c: hs[:, c, :], "hs", (t + 1) % 2)

                st_ = []
                if t >= 1:
                    st_.append((t - 1, FB[t - 1]))
                if t + 1 < NSUB:
                    st_.append((t + 1, (t + 1) % 2))
                mixer_subtile(s, t, post_attn=nxt, stats=st_, do_norm=(t == 0))
            S.alias(FFN_KEYS, MIX_KEYS)
            S.alias(LINK, KVK)
            for t in range(NSUB):
                if t == NSUB - 1:
                    norm_rstd(NSUB - 1, FB[NSUB - 1])
                norm_apply(t, 2, hT_out(t), ("h", t), FB[t])
            if s + 1 < SEQ_PER_CORE:
                def fin(t, s=s):
                    store_subtile(s, t, after_tile=lambda jj: (load_tile(s + 1, t, jj - 1) if jj >= 1 else None))
                    load_tile(s + 1, t, 3)
                    if t >= 1:
                        norm_to(t - 1, 0, hT_out(t - 1), ("h", t - 1), t - 1)
                    if t + 1 < NSUB:
                        load_dma(s + 1, t + 1)

                def bstep(g, t, s=s):
                    if g == NG - 1 and t == 0:
                        load_dma(s + 1, 0)
            else:
                def fin(t, s=s):
                    store_subtile(s, t)
                bstep = None
            ffn_stage(s, 1, on_final=fin, before_step=bstep)
            if s + 1 < SEQ_PER_CORE:
                norm_to(NSUB - 1, 0, hT_out(NSUB - 1), ("h", NSUB - 1), NSUB - 1)

        n_out0 = S.dma_count["dma_out0"]
        n_out1 = S.dma_count["dma_out1"]
        with nc.Block() as block:
            def emit(name, e):
                for (eng, fn, waits, inc) in S.ops:
                    if eng != name:
                        continue
                    for semkey, val in waits:
                        e.wait_ge(sems[semkey], val)
                    fn(e).then_inc(sems[inc[0]], inc[1])

            @block.tensor
            def _(e):
                emit("pe", e)

            @block.scalar
            def _(e):
                emit("act", e)

            @block.vector
            def _(e):
                emit("dve", e)

            @block.gpsimd
            def _(e):
                emit("pool", e)

            @block.sync
            def _(e):
                emit("sp", e)
                e.wait_ge(sems["dma_out0"], n_out0)
                e.wait_ge(sems["dma_out1"], n_out1)
    return nc


_CACHE = {}


def kernel(x, ffn1_norm, ffn1_w_gate, ffn1_w_up, ffn1_w_down, mix_norm, w_in, rel_bias, sinks,
           w_proj_a, w_proj_b, w_out, ffn2_norm, ffn2_w_gate, ffn2_w_up, ffn2_w_down, final_norm):
    f = lambda a: np.asarray(a, dtype=np.float32)
    x = f(x)
    wf1 = _ffn_pieces(f(ffn1_w_gate), f(ffn1_w_up), f(ffn1_w_down))
    wf2 = _ffn_pieces(f(ffn2_w_gate), f(ffn2_w_up), f(ffn2_w_down))
    wmx = _mixer_pieces(f(w_in), f(w_proj_a), f(w_proj_b), f(w_out))
    gains = np.ascontiguousarray(np.concatenate([_gain(f(ffn1_norm)), _gain(f(mix_norm)), _gain(f(ffn2_norm))],
                                                axis=1))
    gfin = np.ascontiguousarray(np.broadcast_to(f(final_norm)[None, :], (128, D)))
    ta, ca, tb = _bias_tables(f(rel_bias))
    sk = f(sinks)
    snk = np.zeros((128, 4), np.float32)
    for m_ in range(4):
        snk[0:64, m_] = sk[2 * m_]
        snk[64:128, m_] = sk[2 * m_ + 1]
    if "nc" not in _CACHE:
        _CACHE["nc"] = build_program()
    nc = _CACHE["nc"]
    shared = {"wf1": wf1, "wf2": wf2, "wmx": wmx, "gains": gains, "gfin": gfin,
              "ta": np.ascontiguousarray(ta.reshape(128, -1)), "ca": ca,
              "tb": np.ascontiguousarray(tb.reshape(128, -1)), "snk": snk}
    in_maps = []
    for c in range(NCORES):
        m = dict(shared)
        m["x"] = np.ascontiguousarray(x[c * SEQ_PER_CORE:(c + 1) * SEQ_PER_CORE])
        in_maps.append(m)
    res = run_bass_kernel_spmd(nc, in_maps, core_ids=list(range(NCORES)))
    out = np.concatenate([np.asarray(r["y"], dtype=np.float32) for r in res.results], axis=0)
    return out
```
